# Optimizing a Trainium2 kernel written in Bass

```python
import math
import jax, jax.numpy as jnp
from jax import lax
import numpy as np

D_MODEL = 2048
BATCH = 2
SEQ = 16384
DEPTH = 1

ATTN_HEAD_DIM = 64
ATTN_WIDTH = D_MODEL // 2
ATTN_HEADS = ATTN_WIDTH // ATTN_HEAD_DIM
ATTN_KV_HEADS = ATTN_HEADS // 4
ATTN_GROUP = ATTN_HEADS // ATTN_KV_HEADS
ATTN_KV_COLS = ATTN_KV_HEADS * ATTN_HEAD_DIM
WINDOW = 128
BLOCK = 128
HALO = -(-WINDOW // BLOCK)
N_BUCKETS = 32
MAX_DISTANCE = 128

RET_WIDTH = D_MODEL - ATTN_WIDTH
RET_HEADS = 8
RET_VAL_DIM = RET_WIDTH // RET_HEADS
RET_KEY_DIM = RET_VAL_DIM // 2
RET_QK_COLS = RET_HEADS * RET_KEY_DIM
CHUNK = 128
ROPE_BASE = 10000.0

MIX_WIDTH = ATTN_WIDTH + RET_WIDTH
SPLIT_SIZES = (ATTN_WIDTH, ATTN_KV_COLS, ATTN_KV_COLS, ATTN_WIDTH,
               RET_QK_COLS, RET_QK_COLS, RET_WIDTH, RET_WIDTH)
IN_WIDTH = sum(SPLIT_SIZES)
EPS = 1e-6

kernel_name = 'hymba_swa_retention_block'


def rmsnorm(x, w):
    x32 = x.astype(jnp.float32)
    x32 = x32 * lax.rsqrt(jnp.mean(x32 * x32, axis=-1, keepdims=True) + EPS)
    return x32.astype(x.dtype) * w


def t5_bucket(rel):
    nb = N_BUCKETS // 2
    max_exact = nb // 2
    ret = jnp.where(rel > 0, nb, 0)
    n = jnp.abs(rel)
    nf = jnp.maximum(n, 1).astype(jnp.float32)
    large = max_exact + (jnp.log(nf / max_exact) / math.log(MAX_DISTANCE / max_exact)
                         * (nb - max_exact)).astype(jnp.int32)
    large = jnp.minimum(large, nb - 1)
    return ret + jnp.where(n < max_exact, n, large)


def windowed_attention(q, k, v, sink, rel_bias):
    B, S = q.shape[0], q.shape[1]
    nb = S // BLOCK
    KW = (2 * HALO + 1) * BLOCK
    qb = q.reshape(B, nb, BLOCK, ATTN_KV_HEADS, ATTN_GROUP, ATTN_HEAD_DIM)
    pad = ((0, 0), (HALO * BLOCK, HALO * BLOCK), (0, 0), (0, 0))
    kp = jnp.pad(k, pad).reshape(B, nb + 2 * HALO, BLOCK, ATTN_KV_HEADS, ATTN_HEAD_DIM)
    vp = jnp.pad(v, pad).reshape(B, nb + 2 * HALO, BLOCK, ATTN_KV_HEADS, ATTN_HEAD_DIM)
    kb = jnp.concatenate([kp[:, o:o + nb] for o in range(2 * HALO + 1)], axis=2)
    vb = jnp.concatenate([vp[:, o:o + nb] for o in range(2 * HALO + 1)], axis=2)

    scale = ATTN_HEAD_DIM ** -0.5
    s = jnp.einsum('bnqhgd,bnkhd->bhgnqk', qb, kb).astype(jnp.float32) * scale

    qi = jnp.arange(BLOCK, dtype=jnp.int32)[:, None]
    kt = jnp.arange(KW, dtype=jnp.int32)[None, :]
    rel = kt - HALO * BLOCK - qi
    bias = rel_bias.astype(jnp.float32)[t5_bucket(rel)]
    bias = jnp.moveaxis(bias, -1, 0).reshape(ATTN_KV_HEADS, ATTN_GROUP, 1, BLOCK, KW)
    key_pos = (jnp.arange(nb, dtype=jnp.int32)[:, None, None] * BLOCK
               + kt[None] - HALO * BLOCK)
    valid = (jnp.abs(rel)[None] <= WINDOW) & (key_pos >= 0) & (key_pos < S)
    s = jnp.where(valid[None, None, None], s + bias[None], -1e30)

    sink_b = sink.astype(jnp.float32).reshape(1, ATTN_KV_HEADS, ATTN_GROUP, 1, 1, 1)
    m = jnp.maximum(jnp.max(s, axis=-1, keepdims=True), sink_b)
    p = jnp.exp(s - m)
    p = p / (jnp.sum(p, axis=-1, keepdims=True) + jnp.exp(sink_b - m))
    out = jnp.einsum('bhgnqk,bnkhd->bnqhgd', p.astype(v.dtype), vb)
    return out.reshape(B, S, ATTN_WIDTH)


def rotary(x):
    S, D = x.shape[1], x.shape[-1]
    inv = ROPE_BASE ** (-jnp.arange(0, D, 2, dtype=jnp.float32) / D)
    ang = jnp.arange(S, dtype=jnp.float32)[:, None] * inv[None, :]
    cos = jnp.cos(ang)[None, :, None, :]
    sin = jnp.sin(ang)[None, :, None, :]
    xr = x.astype(jnp.float32).reshape(x.shape[:-1] + (D // 2, 2))
    x1, x2 = xr[..., 0], xr[..., 1]
    return jnp.stack([x1 * cos - x2 * sin, x1 * sin + x2 * cos], axis=-1).reshape(x.shape)


def retention_chunkwise(q, k, v, log_gamma, strict):
    B, S, H, Dk = q.shape
    Dv = v.shape[-1]
    nc = S // CHUNK
    qc = q.reshape(B, nc, CHUNK, H, Dk)
    kc = k.reshape(B, nc, CHUNK, H, Dk)
    vc = v.reshape(B, nc, CHUNK, H, Dv)
    idx = jnp.arange(CHUNK, dtype=jnp.float32)
    diff = idx[:, None] - idx[None, :]
    keep = diff > 0 if strict else diff >= 0
    dmask = jnp.where(keep[None], jnp.exp(log_gamma[:, None, None] * jnp.maximum(diff, 0.0)[None]), 0.0)
    s = jnp.einsum('bnihd,bnjhd->bhnij', qc, kc) * dmask[None, :, None]
    intra = jnp.einsum('bhnij,bnjhe->bnihe', s, vc)
    kdec = jnp.exp(log_gamma[None, :] * (CHUNK - 1 - idx)[:, None])
    kv = jnp.einsum('bnjhd,bnjhe->nbhde', kc * kdec[None, None, :, :, None], vc)
    chunk_decay = jnp.exp(log_gamma * CHUNK)[None, :, None, None]

    def step(state, kv_c):
        return state * chunk_decay + kv_c, state

    _, prev = lax.scan(step, jnp.zeros((B, H, Dk, Dv), jnp.float32), kv)
    qdec = jnp.exp(log_gamma[None, :] * (idx + 1.0)[:, None])
    cross = jnp.einsum('bnihd,nbhde->bnihe', qc * qdec[None, None, :, :, None], prev)
    return (intra + cross).reshape(B, S, H, Dv)


def bidirectional_retention(q, k, v, decay_fwd, decay_bwd):
    lg_f = jax.nn.log_sigmoid(decay_fwd.astype(jnp.float32))
    lg_b = jax.nn.log_sigmoid(decay_bwd.astype(jnp.float32))
    fwd = retention_chunkwise(q, k, v, lg_f, strict=False)
    flip = lambda t: jnp.flip(t, axis=1)
    bwd = flip(retention_chunkwise(flip(q), flip(k), flip(v), lg_b, strict=True))
    return fwd + bwd


def head_groupnorm(o, w):
    mu = jnp.mean(o, axis=-1, keepdims=True)
    var = jnp.mean((o - mu) ** 2, axis=-1, keepdims=True)
    o = (o - mu) * lax.rsqrt(var + EPS)
    return o.reshape(o.shape[0], o.shape[1], -1) * w.astype(jnp.float32)


def setup_inputs(seed: int = 0) -> dict:
    key = jax.random.key(seed)
    ks = jax.random.split(key, 12)
    f32 = jnp.float32
    x = jax.random.normal(ks[0], (BATCH, SEQ, D_MODEL), f32)
    norm_w = 1.0 + 0.02 * jax.random.normal(ks[1], (DEPTH, D_MODEL), f32)
    w_in = jax.random.normal(ks[2], (DEPTH, D_MODEL, IN_WIDTH), f32) * D_MODEL ** -0.5
    attn_sink = 0.5 * jax.random.normal(ks[3], (DEPTH, ATTN_HEADS), f32)
    rel_bias = 0.1 * jax.random.normal(ks[4], (N_BUCKETS, ATTN_HEADS), f32)
    attn_out_norm_w = 1.0 + 0.02 * jax.random.normal(ks[5], (DEPTH, ATTN_WIDTH), f32)
    scales = jnp.log(2.0 ** (5.0 + jnp.arange(RET_HEADS, dtype=f32)) - 1.0)
    ret_decay_fwd = scales[None] + 0.05 * jax.random.normal(ks[6], (DEPTH, RET_HEADS), f32)
    ret_decay_bwd = scales[None] + 0.05 * jax.random.normal(ks[7], (DEPTH, RET_HEADS), f32)
    ret_gn_w = 1.0 + 0.02 * jax.random.normal(ks[8], (DEPTH, RET_WIDTH), f32)
    w_out = jax.random.normal(ks[9], (DEPTH, MIX_WIDTH, D_MODEL), f32) * MIX_WIDTH ** -0.5
    final_norm_w = 1.0 + 0.02 * jax.random.normal(ks[10], (D_MODEL,), f32)
    return {'x': x, 'norm_w': norm_w, 'w_in': w_in, 'attn_sink': attn_sink,
            'rel_bias': rel_bias, 'attn_out_norm_w': attn_out_norm_w,
            'ret_decay_fwd': ret_decay_fwd, 'ret_decay_bwd': ret_decay_bwd,
            'ret_gn_w': ret_gn_w, 'w_out': w_out, 'final_norm_w': final_norm_w}


def reference(x, norm_w, w_in, attn_sink, rel_bias, attn_out_norm_w,
              ret_decay_fwd, ret_decay_bwd, ret_gn_w, w_out, final_norm_w):
    B, S = x.shape[0], x.shape[1]
    points = [int(p) for p in np.cumsum(SPLIT_SIZES)[:-1]]
    for l in range(DEPTH):
        h = rmsnorm(x, norm_w[l])
        proj = jnp.einsum('bsd,de->bse', h, w_in[l])
        q_a, k_a, v_a, g_a, q_r, k_r, v_r, g_r = jnp.split(proj, points, axis=-1)

        attn = windowed_attention(q_a.reshape(B, S, ATTN_HEADS, ATTN_HEAD_DIM),
                                  k_a.reshape(B, S, ATTN_KV_HEADS, ATTN_HEAD_DIM),
                                  v_a.reshape(B, S, ATTN_KV_HEADS, ATTN_HEAD_DIM),
                                  attn_sink[l], rel_bias)
        attn = rmsnorm(attn, attn_out_norm_w[l]) * jax.nn.silu(g_a)

        qr = rotary(q_r.reshape(B, S, RET_HEADS, RET_KEY_DIM))
        kr = rotary(k_r.reshape(B, S, RET_HEADS, RET_KEY_DIM)) * RET_KEY_DIM ** -0.5
        vr = v_r.reshape(B, S, RET_HEADS, RET_VAL_DIM).astype(jnp.float32)
        ret = bidirectional_retention(qr, kr, vr, ret_decay_fwd[l], ret_decay_bwd[l])
        ret = head_groupnorm(ret, ret_gn_w[l]).astype(x.dtype) * jax.nn.silu(g_r)

        mixed = jnp.concatenate([attn.astype(x.dtype), ret], axis=-1)
        x = x + jnp.einsum('bse,ed->bsd', mixed, w_out[l])
    return rmsnorm(x, final_norm_w)
```

```python
import math
from contextlib import ExitStack

import numpy as np

import concourse.bass as bass
import concourse.mybir as mybir
from concourse.bass_utils import run_bass_kernel_spmd

F32 = mybir.dt.float32
BF16 = mybir.dt.bfloat16
AF = mybir.ActivationFunctionType
ALU = mybir.AluOpType
AX = mybir.AxisListType

D = 2048
KC = 16
NEG = -30000.0
EPS = 1e-6
ENGS = ("pe", "act", "dve", "pool", "sp")


class Sched:
    def __init__(self):
        self.q = {k: [] for k in ENGS}
        self.cnt = dict.fromkeys(ENGS, 0)
        self.seen = {k: {} for k in ENGS}
        self.lw = {}
        self.rd = {}
        self.dcnt = {}

    def _deps(self, eng, reads, writes):
        need = {}

        def add(ev, raw):
            if ev is None:
                return
            k, v = ev
            if k == eng and eng == "pe":
                return
            if need.get(k, 0) < v:
                need[k] = v

        for key in reads:
            add(self.lw.get(key), True)
        for key in writes:
            add(self.lw.get(key), False)
            for ev in self.rd.get(key, ()):
                add(ev, False)
        waits = []
        for k, v in need.items():
            if self.seen[eng].get(k, 0) < v:
                self.seen[eng][k] = v
                waits.append((k, v))
        return waits

    def _commit(self, ev, reads, writes):
        for key in writes:
            self.lw[key] = ev
            self.rd[key] = []
        for key in reads:
            lst = self.rd.setdefault(key, [])
            for i, (k, v) in enumerate(lst):
                if k == ev[0]:
                    lst[i] = (k, max(v, ev[1]))
                    break
            else:
                lst.append(ev)

    def op(self, eng, fn, reads=(), writes=(), inc=True):
        waits = self._deps(eng, reads, writes)
        if inc:
            self.cnt[eng] += 1
            ev = (eng, self.cnt[eng])
        else:
            ev = (eng, self.cnt[eng] + 1)
        self._commit(ev, reads, writes)
        self.q[eng].append((waits, fn, (eng, 1) if inc else None))

    def dma(self, eng, fn, reads, writes, sem):
        waits = self._deps(eng, reads, writes)
        self.dcnt[sem] = self.dcnt.get(sem, 0) + 16
        ev = (sem, self.dcnt[sem])
        self._commit(ev, reads, writes)
        self.q[eng].append((waits, fn, (sem, 16)))

    def cc(self, eng, fn, reads, writes, sem):
        waits = self._deps(eng, reads, writes)
        self.dcnt[sem] = self.dcnt.get(sem, 0) + 1
        ev = (sem, self.dcnt[sem])
        self._commit(ev, reads, writes)
        self.q[eng].append((waits, fn, (sem, 1)))

    def barrier(self):
        evs = [(k, self.cnt[k]) for k in ENGS if self.cnt[k] > 0] + \
              [(k, v) for k, v in self.dcnt.items() if not k.startswith("cast")]
        for eng in ENGS:
            waits = []
            for k, v in evs:
                if k != eng and self.seen[eng].get(k, 0) < v:
                    self.seen[eng][k] = v
                    waits.append((k, v))
            if waits:
                self.q[eng].append((waits, None, None))

    def final_wait(self, eng):
        waits = [(k, v) for k, v in self.dcnt.items() if self.seen[eng].get(k, 0) < v]
        waits += [(k, self.cnt[k]) for k in ENGS if k != eng and self.cnt[k] > 0]
        self.q[eng].append((waits, None, None))

    def emit(self, nc, es):
        keys = [k for k in ENGS if self.cnt[k] > 0] + list(self.dcnt.keys())
        sems = {k: es.enter_context(nc.semaphore("s_" + k)) for k in keys}
        block = es.enter_context(nc.Block())
        deco = {"pe": block.tensor, "act": block.scalar, "dve": block.vector,
                "pool": block.gpsimd, "sp": block.sync}

        def runner(eng):
            def _(e):
                for waits, fn, inc in self.q[eng]:
                    fuse = fn is not None and waits and (inc is None or inc[0] in ENGS)
                    for k, v in (waits[1:] if fuse else waits):
                        e.wait_ge(sems[k], v)
                    if fn is not None:
                        ins = fn(e)
                        if fuse:
                            ins._wait_ge(sems[waits[0][0]], waits[0][1])
                        if inc is not None:
                            if inc[1] == 1 and inc[0] not in ENGS:
                                ins.then_inc(sems[inc[0]])
                            else:
                                ins.then_inc(sems[inc[0]], inc[1])
            return _

        for eng in ENGS:
            if self.q[eng]:
                deco[eng](runner(eng))


STOP = None
EW2 = "dve"


def build(TC):
    NT = TC // 128
    NST = TC // 512
    NCH = NT
    assert TC % 512 == 0
    nc = bass.Bass("TRN2", target_bir_lowering=False)

    def din(name, shape):
        return nc.dram_tensor(name, shape, F32, kind="ExternalInput").ap()

    x_d = din("x", [TC + 256, D])
    wA_d = din("wA", [D, 4096])
    wKV_d = din("wKV", [D, 1536])
    wO_d = din("wO", [D, D])
    rot_d = din("rot", [128, 2, NT, 32])
    biasg_d = din("biasg", [128, 3, 16, 128])
    maskc_d = din("maskc", [128, 3, 128])
    edge_d = din("edge", [128, 2])
    normw_d = din("normw_t", [128, 16])
    aonw_d = din("aonw_t", [128, 8])
    gnw_d = din("gnw_t", [128, 8])
    fnw_d = din("fnw", [D])
    sink_d = din("sink", [16])
    dec_d = din("dec", [16])
    cc_d = din("cc", [32])
    y_d = nc.dram_tensor("y", [TC, D], F32, kind="ExternalOutput").ap()

    wKVb = nc.dram_tensor("wKVb", [3, 128, KC, 512], BF16).ap()
    wAb = nc.dram_tensor("wAb", [8, 128, KC, 512], BF16).ap()
    wOb = nc.dram_tensor("wOb", [4, 128, KC, 512], BF16).ap()
    kscr = nc.dram_tensor("kscr", [NT, 128, 512], BF16).ap()
    vscr = nc.dram_tensor("vscr", [NT, 128, 1024], BF16).ap()
    sbscr = nc.dram_tensor("sbscr", [NT, 128, 512], BF16).ap()
    ccin_t = nc.dram_tensor("ccin", [128, 1024], F32)
    ccout_t = nc.dram_tensor("ccout", [8 * 128, 1024], F32)
    ccin = ccin_t.ap()
    ccout = ccout_t.ap()

    es = ExitStack()
    S = Sched()

    def finish():
        S.final_wait("pool")
        S.emit(nc, es)
        es.close()
        return nc

    def sb(name, shape, dt=F32):
        return es.enter_context(nc.sbuf_tensor("sb_" + name, shape, dt))

    def pt(name, shape, dt=F32):
        return es.enter_context(nc.psum_tensor("ps_" + name, shape, dt))

    ident = sb("ident", [128, 128], BF16)
    W = [sb("W0", [128, KC, 512], BF16), sb("W1", [128, KC, 512], BF16)]
    hT = sb("hT", [128, KC, 768], BF16)
    xs = [sb("xs0", [128, D]), sb("xs1", [128, D])]
    xn = sb("xn", [128, D], BF16)
    junk = sb("junk", [128, D], BF16)
    biasT = sb("biasT", [128, 3, 16, 128])
    mixT = sb("mixT", [128, KC, 512], BF16)
    fnw = sb("fnw", [128, D])
    rot = sb("rot", [128, 2, NT, 32])
    dmaskT = sb("dmaskT", [128, 8, 128])
    UW = 23040
    U = sb("U", [128, UW], BF16)
    normw = sb("normw", [128, 16])
    aonw = sb("aonw", [128, 8])
    gnw = sb("gnw", [128, 8])
    edge = sb("edge", [128, 2])
    esink = sb("esink", [128, 16])
    dec16 = sb("dec16", [128, 16])
    lgA = sb("lgA", [128, 16])
    lgP = sb("lgP", [128, 16])
    lgC = sb("lgC", [128, 8])
    cct = sb("cct", [128, 32])
    qdecf = sb("qdecf", [128, 8])
    qdecb = sb("qdecb", [128, 8])
    kdecf = sb("kdecf", [128, 8])
    kdecb8 = sb("kdecb8", [128, 8])
    kdecfc8 = sb("kdecfc8", [128, NCH, 8])
    pwb = sb("pwb", [128, NCH, 4])
    DC = sb("DC", [128, 8])
    Wc = sb("Wc", [128, 2, 8, 4])
    stat = sb("stat", [128, 64])
    stat8 = sb("stat8", [128, 128])
    Tf32 = sb("Tf32", [128, 4, 128])
    Sb32 = sb("Sb32", [128, 4, 128])
    Sf32 = sb("Sf32", [128, 4, 128])
    Sin32 = sb("Sin32", [128, 2, 4, 128])
    Sfb = [sb("Sfb0", [128, 4, 128], BF16), sb("Sfb1", [128, 4, 128], BF16)]
    sbc = [sb("sbc0", [128, 4, 128], BF16), sb("sbc1", [128, 4, 128], BF16)]

    G = [pt("G0", [128, 512]), pt("G1", [128, 512])]
    T = [pt("T0", [128, 8, 128], BF16), pt("T1", [128, 8, 128], BF16)]
    A = [pt("A%d" % i, [128, 512]) for i in range(4)]

    def uv(off, n, dt=BF16):
        assert off % 4 == 0
        if dt == BF16:
            assert off + 2 * n <= 2 * UW
            return U[:, off // 2: off // 2 + n]
        assert off + 4 * n <= 2 * UW
        return U[:, off // 2: off // 2 + 2 * n].bitcast(F32)

    _statn = [0]

    def col():
        i = _statn[0] % 64
        _statn[0] += 1
        return stat[:, i:i + 1], "stat%d" % i

    _stat8n = [0]

    def col8():
        i = (_stat8n[0] % 16) * 8
        _stat8n[0] += 1
        return stat8[:, i:i + 8], ["s8_%d" % j for j in range(i, i + 8)]

    for cb in range(3):
        S.dma("pool", lambda e, cb=cb: e.dma_start(
            out=wKVb[cb], in_=wKV_d[:, cb * 512:(cb + 1) * 512].rearrange("(k p) n -> p k n", p=128)),
            [], ["wKVb"], "castkv")
    for cb in range(8):
        S.dma("pool", lambda e, cb=cb: e.dma_start(
            out=wAb[cb], in_=wA_d[:, cb * 512:(cb + 1) * 512].rearrange("(k p) n -> p k n", p=128)),
            [], ["wAb"], "casta")
    for cb in range(4):
        S.dma("pool", lambda e, cb=cb: e.dma_start(
            out=wOb[cb], in_=wO_d[:, cb * 512:(cb + 1) * 512].rearrange("(k p) n -> p k n", p=128)),
            [], ["wOb"], "casto")

    def ld(dst, src, key):
        S.dma("sp", lambda e: e.dma_start(out=dst, in_=src), [], [key], "ldm")

    ld(rot[:], rot_d, "rot")
    ld(biasT[:], biasg_d, "biasT")
    maskc = uv(0, 384, F32).rearrange("p (a b) -> p a b", a=3)
    ld(maskc, maskc_d, "maskc")
    ld(edge[:], edge_d, "edge")
    ld(normw[:], normw_d, "normw")
    ld(aonw[:], aonw_d, "aonw")
    ld(gnw[:], gnw_d, "gnw")
    ld(fnw[:], fnw_d.partition_broadcast(128), "fnw")
    ld(esink[:], sink_d.partition_broadcast(128), "esink")
    ld(dec16[:], dec_d.partition_broadcast(128), "dec16")
    ld(cct[:], cc_d.partition_broadcast(128), "cct")
    for key_ in ("rot", "biasT", "maskc", "edge", "normw", "aonw", "gnw", "fnw", "esink", "dec16", "cct"):
        S.lw[key_] = ("ldm", S.dcnt["ldm"])

    S.op("pool", lambda e: e.memset(ident[:], 0.0), [], ["ident"])
    S.op("pool", lambda e: e.affine_select(out=ident[:], in_=ident[:], pattern=[[-1, 128]],
                                           compare_op=ALU.not_equal, fill=1.0, base=0,
                                           channel_multiplier=1), ["ident"], ["ident"])
    for kb in range(3):
        S.op("dve", lambda e, kb=kb: e.tensor_tensor(
            out=biasT[:, kb], in0=biasT[:, kb],
            in1=maskc[:, kb, :].unsqueeze(1).to_broadcast([128, 16, 128]), op=ALU.add),
            ["biasT", "maskc"], ["biasT"])
    S.op("act", lambda e: e.activation(out=esink[:], in_=esink[:], func=AF.Exp), ["esink"], ["esink"])

    e1 = uv(2048, 16, F32)
    tt = uv(2048 + 64, 16, F32)
    S.op("act", lambda e: e.activation(out=e1, in_=dec16[:], func=AF.Exp, scale=-1.0), ["dec16"], ["e1"])
    S.op("dve", lambda e: e.tensor_scalar(out=tt, in0=e1, scalar1=-1.0 / 6, scalar2=1.0 / 5,
                                          op0=ALU.mult, op1=ALU.add), ["e1"], ["tt"])
    for cst in (1.0 / 4, 1.0 / 3, 1.0 / 2, 1.0):
        S.op("dve", lambda e: e.tensor_tensor(out=tt, in0=tt, in1=e1, op=ALU.mult), ["tt", "e1"], ["tt"])
        S.op("dve", lambda e, cst=cst: e.tensor_scalar(out=tt, in0=tt, scalar1=-1.0, scalar2=cst,
                                                       op0=ALU.mult, op1=ALU.add), ["tt"], ["tt"])
    S.op("dve", lambda e: e.tensor_tensor(out=tt, in0=tt, in1=e1, op=ALU.mult), ["tt", "e1"], ["tt"])
    S.op("dve", lambda e: e.tensor_single_scalar(out=lgA[:], in_=tt, scalar=-1.0, op=ALU.mult),
         ["tt"], ["lgA"])
    for d in range(2):
        S.op("dve", lambda e, d=d: e.tensor_copy(
            out=lgP[:, d * 8:(d + 1) * 8].rearrange("p (m hh) -> p hh m", hh=2),
            in_=lgA[:, d * 8:(d + 1) * 8].rearrange("p (hh m) -> p hh m", m=4)), ["lgA"], ["lgP"])
        for hh in range(2):
            S.op("dve", lambda e, d=d, hh=hh: e.tensor_copy(
                out=lgC[hh * 64:(hh + 1) * 64, d * 4:(d + 1) * 4],
                in_=lgA[hh * 64:(hh + 1) * 64, d * 8 + hh * 4:d * 8 + hh * 4 + 4]), ["lgA"], ["lgC"])

    cols = uv(2304, 4, F32)
    for i, (base, cm) in enumerate(((1, 1), (128, -1), (127, -1), (0, 1))):
        S.op("pool", lambda e, i=i, base=base, cm=cm: e.iota(
            out=cols[:, i:i + 1], pattern=[[0, 1]], base=base, channel_multiplier=cm,
            allow_small_or_imprecise_dtypes=True), [], ["cols"])
    argt = uv(2560, 32, F32)
    for i, (dst, lo) in enumerate(((qdecf, 0), (qdecb, 8), (kdecf, 0), (kdecb8, 8))):
        S.op("dve", lambda e, i=i, lo=lo: e.tensor_single_scalar(out=argt[:, i * 8:(i + 1) * 8], in_=lgP[:, lo:lo + 8], scalar=cols[:, i:i + 1], op=ALU.mult), ["lgP", "cols"], ["argt"])
    for i, dst in enumerate((qdecf, qdecb, kdecf, kdecb8)):
        S.op("act", lambda e, i=i, dst=dst: e.activation(out=dst[:], in_=argt[:, i * 8:(i + 1) * 8], func=AF.Exp),
             ["argt"], ["dt%d" % i])
    S.op("dve", lambda e: e.tensor_single_scalar(out=kdecb8[:], in_=kdecb8[:], scalar=0.125, op=ALU.mult),
         ["dt3"], ["dt3"])
    colc = uv(2816, NCH, F32)
    S.op("pool", lambda e: e.iota(out=colc, pattern=[[-128, NCH]], base=127 + 128 * (NCH - 1),
                                  channel_multiplier=-1, allow_small_or_imprecise_dtypes=True), [], ["colc"])
    S.op("dve", lambda e: e.tensor_tensor(
        out=kdecfc8[:], in0=lgP[:, 0:8].unsqueeze(1).to_broadcast([128, NCH, 8]),
        in1=colc.unsqueeze(2).to_broadcast([128, NCH, 8]), op=ALU.mult), ["lgP", "colc"], ["kdecfc8"])
    S.op("act", lambda e: e.activation(out=kdecfc8[:], in_=kdecfc8[:], func=AF.Exp), ["kdecfc8"], ["kdecfc8"])
    S.op("dve", lambda e: e.tensor_single_scalar(out=kdecfc8[:], in_=kdecfc8[:], scalar=0.125, op=ALU.mult), ["kdecfc8"], ["kdecfc8"])
    cval = uv(3072, NCH, F32)
    S.op("pool", lambda e: e.iota(out=cval, pattern=[[-128, NCH]], base=128 * (NCH - 1), channel_multiplier=0,
                                  allow_small_or_imprecise_dtypes=True), [], ["cval"])
    S.op("dve", lambda e: e.tensor_tensor(
        out=pwb[:], in0=lgC[:, 4:8].unsqueeze(1).to_broadcast([128, NCH, 4]),
        in1=cval.unsqueeze(2).to_broadcast([128, NCH, 4]), op=ALU.mult), ["lgC", "cval"], ["pwb"])
    S.op("act", lambda e: e.activation(out=pwb[:], in_=pwb[:], func=AF.Exp), ["pwb"], ["pwb"])
    S.op("act", lambda e: e.activation(out=DC[:], in_=lgC[:], func=AF.Exp, scale=128.0), ["lgC"], ["DC"])
    S.op("dve", lambda e: e.tensor_single_scalar(out=cct[:, 0:16], in_=cct[:, 0:16], scalar=float(128 * NCH), op=ALU.mult), ["cct"], ["cct"])
    S.op("dve", lambda e: e.tensor_tensor(
        out=Wc[:], in0=lgC[:].rearrange("p (d m) -> p d m", d=2).unsqueeze(2).to_broadcast([128, 2, 8, 4]),
        in1=cct[:, 0:16].rearrange("p (d s) -> p d s", d=2).unsqueeze(3).to_broadcast([128, 2, 8, 4]),
        op=ALU.mult), ["lgC", "cct"], ["Wc"])
    S.op("act", lambda e: e.activation(out=Wc[:], in_=Wc[:], func=AF.Exp), ["Wc"], ["Wc"])
    S.op("dve", lambda e: e.tensor_tensor(
        out=Wc[:], in0=Wc[:],
        in1=cct[:, 16:32].rearrange("p (d s) -> p d s", d=2).unsqueeze(3).to_broadcast([128, 2, 8, 4]),
        op=ALU.mult), ["Wc", "cct"], ["Wc"])
    diff = uv(4096, 128, F32)
    dd1 = uv(4096 + 512, 128, F32)
    dd2 = uv(4096 + 1024, 128, F32)
    dtmp = uv(4096 + 1536, 128, F32)
    S.op("pool", lambda e: e.iota(out=diff, pattern=[[1, 128]], base=0, channel_multiplier=-1,
                                  allow_small_or_imprecise_dtypes=True), [], ["diff"])
    S.op("dve", lambda e: e.tensor_single_scalar(out=dd1, in_=diff, scalar=0.0, op=ALU.max),
         ["diff"], ["dd1"])
    S.op("dve", lambda e: e.tensor_scalar(out=dd2, in0=diff, scalar1=-1.0, scalar2=0.0, op0=ALU.mult, op1=ALU.max),
         ["diff"], ["dd2"])
    for h in range(8):
        S.op("dve", lambda e, h=h: e.tensor_single_scalar(out=dtmp, in_=dd1, scalar=lgA[:, h:h + 1], op=ALU.mult), ["dd1", "lgA"], ["dtmp"])
        S.op("dve", lambda e, h=h: e.scalar_tensor_tensor(out=dtmp, in0=dd2, scalar=lgA[:, 8 + h:9 + h], in1=dtmp,
                                                          op0=ALU.mult, op1=ALU.add), ["dd2", "lgA", "dtmp"], ["dtmp"])
        S.op("act", lambda e, h=h: e.activation(out=dmaskT[:, h, :], in_=dtmp, func=AF.Exp), ["dtmp"], ["dmaskT"])
    S.op("pool", lambda e: e.memset(Tf32[:], 0.0), [], ["Tf32"])
    S.op("pool", lambda e: e.memset(Sb32[:], 0.0), [], ["Sb32"])
    S.barrier()
    if STOP == "setup":
        return finish()

    wseq = []
    for st in reversed(range(NST)):
        wseq += [("wKVb", wKVb[cb]) for cb in range(3)]
    for st in range(NST):
        wseq += [("wAb", wAb[cb]) for cb in range(8)] + [("wOb", wOb[cb]) for cb in range(4)]
    wpos = [0]

    def wload(i):
        if i < len(wseq):
            key, ap = wseq[i]
            S.dma("sp", lambda e: e.dma_start(out=W[i % 2][:], in_=ap), [key], ["W%d" % (i % 2)], "ldw%d" % (i % 2))

    def wnext():
        i = wpos[0]
        wpos[0] += 1
        wload(i + 1)
        return W[i % 2], "W%d" % (i % 2)

    wload(0)

    xcnt = [0]

    def xload(row0):
        xi = xcnt[0] % 2
        xcnt[0] += 1
        S.dma("sp", lambda e: e.dma_start(out=xs[xi][:], in_=x_d[row0:row0 + 128, :]), [], ["xs%d" % xi], "ldx%d" % xi)
        return xi

    def rstd_from(ssq, ssk, n):
        sd, sdk = col()
        S.op("act", lambda e: e.activation(out=sd, in_=ssq, func=AF.Sqrt, scale=1.0 / n, bias=EPS), [ssk], [sdk])
        rs, rsk = col()
        S.op("dve", lambda e: e.reciprocal(out=rs, in_=sd), [sdk], [rsk])
        return rs, rsk

    def rms_tile(xi, hcol):
        xk = "xs%d" % xi
        ssq, ssk = col()
        S.op("act", lambda e: e.activation(out=junk[:], in_=xs[xi][:], func=AF.Square, accum_out=ssq), [xk], [ssk])
        rs, rsk = rstd_from(ssq, ssk, D)
        S.op("dve", lambda e: e.tensor_single_scalar(out=xn[:], in_=xs[xi][:], scalar=rs, op=ALU.mult),
             [xk, rsk], ["xn"])
        for half in range(2):
            for k in range(8):
                kc = half * 8 + k
                S.op("pe", lambda e, half=half, k=k, kc=kc: e.transpose(
                    out=T[half][:, k, :], in_=xn[:, kc * 128:(kc + 1) * 128], identity=ident[:]),
                    ["xn", "ident"], ["T%d" % half], inc=(k == 7))
            S.op("dve", lambda e, half=half: e.tensor_tensor(
                out=hT[:, half * 8:(half + 1) * 8, hcol * 128:(hcol + 1) * 128], in0=T[half][:],
                in1=normw[:, half * 8:(half + 1) * 8].unsqueeze(2).to_broadcast([128, 8, 128]), op=ALU.mult),
                ["T%d" % half, "normw"], ["hT%d" % hcol])

    gcnt = [0]

    def gemm_tok(Wb, wk, c0, ncols, hcol):
        gi = gcnt[0] % 2
        gcnt[0] += 1
        for kc in range(KC):
            S.op("pe", lambda e, kc=kc: e.matmul(
                out=G[gi][:, 0:ncols], lhsT=hT[:, kc, hcol * 128:(hcol + 1) * 128], rhs=Wb[:, kc, c0:c0 + ncols],
                start=(kc == 0), stop=(kc == KC - 1)), [wk, "hT%d" % hcol], ["G%d" % gi], inc=(kc == KC - 1))
        return G[gi], "G%d" % gi

    def gemm_feat(Wb, wk, c0, t0, ntok, hkeys):
        gi = gcnt[0] % 2
        gcnt[0] += 1
        for kc in range(KC):
            S.op("pe", lambda e, kc=kc: e.matmul(
                out=G[gi][:, 0:ntok], lhsT=Wb[:, kc, c0:c0 + 128], rhs=hT[:, kc, t0:t0 + ntok],
                start=(kc == 0), stop=(kc == KC - 1)), [wk] + hkeys, ["G%d" % gi], inc=(kc == KC - 1))
        return G[gi], "G%d" % gi

    def rotary(src, srck, dst, dstk, tile_idx, eng="dve"):
        tv = [uv(ROT_OFF + k * 1024, 256, F32).rearrange("p (m hh i) -> p m hh i", m=4, hh=2) for k in range(4)]
        sv = src.rearrange("p (hh m i two) -> p m hh i two", hh=2, m=4, two=2)
        x1 = sv[:, :, :, :, 0]
        x2 = sv[:, :, :, :, 1]
        cosb = rot[:, 0, tile_idx, :].unsqueeze(1).unsqueeze(1).to_broadcast([128, 4, 2, 32])
        sinb = rot[:, 1, tile_idx, :].unsqueeze(1).unsqueeze(1).to_broadcast([128, 4, 2, 32])
        dv = dst.rearrange("p (m hh) d -> p m hh d", hh=2)
        o1 = dv[:, :, :, 0:32]
        o2 = dv[:, :, :, 32:64]
        S.op(eng, lambda e: e.tensor_tensor(out=tv[0], in0=x1, in1=cosb, op=ALU.mult), [srck, "rot"], ["rt0"])
        S.op(eng, lambda e: e.tensor_tensor(out=tv[1], in0=x2, in1=sinb, op=ALU.mult), [srck, "rot"], ["rt1"])
        S.op(eng, lambda e: e.tensor_tensor(out=tv[2], in0=x1, in1=sinb, op=ALU.mult), [srck, "rot"], ["rt2"])
        S.op(eng, lambda e: e.tensor_tensor(out=tv[3], in0=x2, in1=cosb, op=ALU.mult), [srck, "rot"], ["rt3"])
        S.op(eng, lambda e: e.tensor_tensor(out=o1, in0=tv[0], in1=tv[1], op=ALU.subtract), ["rt0", "rt1"], [dstk + "a"])
        S.op(eng, lambda e: e.tensor_tensor(out=o2, in0=tv[2], in1=tv[3], op=ALU.add), ["rt2", "rt3"], [dstk + "b"])

    def pi_bcast(tab8):
        return tab8.rearrange("p (m hh) -> p m hh", hh=2).unsqueeze(3).to_broadcast([128, 4, 2, 64])

    ROT_OFF = 0
    p1_kraw = uv(4096, 512, F32)
    p1_krot = uv(6144, 512, F32).rearrange("p (a d) -> p a d", a=8)
    v4 = lambda off: uv(off, 512).rearrange("p (m hh d) -> p m hh d", m=4, hh=2)
    p1_kst = [v4(8192 + i * 1024) for i in range(4)]
    p1_kdf = [v4(12288 + i * 1024) for i in range(4)]
    p1_kdb = [v4(16384 + i * 1024) for i in range(4)]
    p1_vb = [uv(20480 + i * 2048, 1024) for i in range(4)]
    p1_sbo = [uv(28672 + i * 1024, 512).rearrange("p (m e) -> p m e", m=4) for i in range(2)]
    p1_tmp = uv(30720, 512, F32).rearrange("p (m e) -> p m e", m=4)
    krot4 = p1_krot.rearrange("p (m hh) d -> p m hh d", hh=2)

    def p1_rms_gen(st_):
        tl = list(reversed(range(4)))
        xi_ = xload(128 + (st_ * 4 + tl[0]) * 128)
        for n, t in enumerate(tl):
            xi_next = xload(128 + (st_ * 4 + tl[n + 1]) * 128) if n + 1 < 4 else None
            rms_tile(xi_, t)
            xi_ = xi_next
            yield

    for st in reversed(range(NST)):
        tiles = list(reversed(range(4)))
        if st == NST - 1:
            for _ in p1_rms_gen(st):
                pass
        nxt1 = p1_rms_gen(st - 1) if st > 0 else None
        Wk, wkk = wnext()
        for t in tiles:
            c = st * 4 + t
            g, gk = gemm_tok(Wk, wkk, 0, 512, t)
            S.op("act", lambda e, g=g: e.activation(out=p1_kraw, in_=g[:], func=AF.Copy), [gk], ["p1_kraw"])
            rotary(p1_kraw, "p1_kraw", p1_krot, "p1_krot", c)
            S.op("dve", lambda e, t=t: e.tensor_single_scalar(out=p1_kst[t], in_=krot4, scalar=0.125, op=ALU.mult),
                 ["p1_krota", "p1_krotb"], ["p1_kst%d" % t])
            S.op("dve", lambda e, t=t, c=c: e.tensor_tensor(out=p1_kdf[t], in0=krot4, in1=pi_bcast(kdecfc8[:, c, :]),
                                                            op=ALU.mult), ["p1_krota", "p1_krotb", "kdecfc8"], ["p1_kdf%d" % t])
            S.op("dve", lambda e, t=t: e.tensor_tensor(out=p1_kdb[t], in0=krot4, in1=pi_bcast(kdecb8[:]),
                                                       op=ALU.mult), ["p1_krota", "p1_krotb", "dt3"], ["p1_kdb%d" % t])
            S.dma("pool", lambda e, t=t, c=c: e.dma_start(
                out=kscr[c], in_=p1_kst[t].rearrange("p m hh d -> p (m hh d)")),
                ["p1_kst%d" % t], ["kscr%d" % c], "stk%d" % t)
        for half in range(2):
            Wv, wvk = wnext()
            for t in tiles:
                g, gk = gemm_tok(Wv, wvk, 0, 512, t)
                S.op("act", lambda e, g=g, t=t, half=half: e.activation(
                    out=p1_vb[t][:, half * 512:(half + 1) * 512], in_=g[:], func=AF.Copy), [gk], ["p1_vb%d" % t])
        for t in tiles:
            c = st * 4 + t
            S.dma("pool", lambda e, t=t, c=c: e.dma_start(out=vscr[c], in_=p1_vb[t]),
                  ["p1_vb%d" % t], ["vscr%d" % c], "stv%d" % t)
        for t in tiles:
            c = st * 4 + t
            b = c % 2
            if nxt1 is not None:
                next(nxt1, None)
            for dirn, kd in ((0, p1_kdf), (1, p1_kdb)):
                for hh in range(2):
                    for m in range(4):
                        h = 4 * hh + m
                        S.op("pe", lambda e, t=t, hh=hh, m=m, h=h, kd=kd, dirn=dirn: e.matmul(
                            out=A[2 * dirn + hh][:, m * 128:(m + 1) * 128],
                            lhsT=kd[t][:, m].rearrange("p hh d -> p (hh d)"),
                            rhs=p1_vb[t][:, h * 128:(h + 1) * 128], start=True, stop=True),
                            ["p1_kd%s%d" % ("fb"[dirn], t), "p1_vb%d" % t], ["A%d" % (2 * dirn + hh)], inc=(m == 3))
            for hh in range(2):
                ps_ = slice(hh * 64, (hh + 1) * 64)
                S.op("dve", lambda e, hh=hh, ps_=ps_: e.tensor_tensor(
                    out=Tf32[ps_], in0=Tf32[ps_], in1=A[hh][ps_, :].rearrange("p (m e) -> p m e", m=4), op=ALU.add),
                    ["Tf32", "A%d" % hh], ["Tf32"])
            S.op("act", lambda e, b=b: e.activation(out=p1_sbo[b], in_=Sb32[:], func=AF.Copy), ["Sb32"], ["p1_sbo%d" % b])
            S.dma("pool", lambda e, b=b, c=c: e.dma_start(out=sbscr[c], in_=p1_sbo[b].rearrange("p m e -> p (m e)")),
                  ["p1_sbo%d" % b], ["sbscr%d" % c], "stb%d" % b)
            S.op("dve", lambda e: e.tensor_tensor(
                out=p1_tmp, in0=Sb32[:], in1=DC[:, 4:8].unsqueeze(2).to_broadcast([128, 4, 128]), op=ALU.mult),
                ["Sb32", "DC"], ["p1_tmp"])
            for hh in range(2):
                ps_ = slice(hh * 64, (hh + 1) * 64)
                S.op("dve", lambda e, hh=hh, ps_=ps_: e.tensor_tensor(
                    out=Sb32[ps_], in0=p1_tmp[ps_], in1=A[2 + hh][ps_, :].rearrange("p (m e) -> p m e", m=4),
                    op=ALU.add), ["p1_tmp", "A%d" % (2 + hh)], ["Sb32"])

    if STOP == "p1":
        return finish()
    S.dma("pool", lambda e: e.dma_start(out=ccin[:, 0:512], in_=Tf32[:].rearrange("p m e -> p (m e)")),
          ["Tf32"], ["ccin"], "cci")
    S.dma("pool", lambda e: e.dma_start(out=ccin[:, 512:1024], in_=Sb32[:].rearrange("p m e -> p (m e)")),
          ["Sb32"], ["ccin"], "cci")
    S.cc("pool", lambda e: e.collective_compute("AllGather", ALU.bypass, replica_groups=[list(range(8))],
                                                ins=[ccin_t.ap().opt()], outs=[ccout_t.ap().opt()]),
         ["ccin"], ["ccout"], "ccs")
    S.op("pool", lambda e: e.memset(Sin32[:], 0.0), [], ["Sin32"])
    S.barrier()
    cctmp = [uv(i * 4096, 1024, F32).rearrange("p (d m e) -> p d m e", d=2, m=4) for i in range(2)]
    for src in range(8):
        bb = src % 2
        S.dma("sp", lambda e, src=src, bb=bb: e.dma_start(
            out=cctmp[bb][:].rearrange("p d m e -> p (d m e)"), in_=ccout[src * 128:(src + 1) * 128, :]),
            ["ccout"], ["cctmp%d" % bb], "ldc%d" % bb)
        for d in range(2):
            for m in range(4):
                S.op("dve", lambda e, src=src, bb=bb, d=d, m=m: e.scalar_tensor_tensor(
                    out=Sin32[:, d, m, :], in0=cctmp[bb][:, d, m, :], scalar=Wc[:, d, src, m:m + 1],
                    in1=Sin32[:, d, m, :], op0=ALU.mult, op1=ALU.add),
                    ["cctmp%d" % bb, "Wc", "Sin32"], ["Sin32"])
    S.op("dve", lambda e: e.tensor_copy(out=Sf32[:], in_=Sin32[:, 0]), ["Sin32"], ["Sf32"])
    S.op("dve", lambda e: e.tensor_copy(out=Sfb[0][:], in_=Sin32[:, 0]), ["Sin32"], ["Sfb0"])
    S.barrier()

    if STOP == "xchg":
        return finish()
    qaT = uv(0, 4096).rearrange("p (c t) -> p c t", c=8)
    kaT = uv(8192, 1536).rearrange("p (c t) -> p c t", c=2)
    va = uv(11264, 1584).rearrange("p (t k d) -> p t k d", t=6, k=4)
    ga = uv(14432, 4096).rearrange("p (t n) -> p t n", t=4)
    tsc = [uv(22624 + i * 2048, 512, F32).rearrange("p (g i) -> p g i", g=4) for i in range(2)]
    PT = [uv(26720 + i * 1024, 512).rearrange("p (g i) -> p g i", g=4) for i in range(6)]
    atto = uv(32864, 1024, F32)
    mixa = uv(36960, 1024)
    gr = uv(0, 4096).rearrange("p (t n) -> p t n", t=4)
    ktok = [uv(8192 + i * 1024, 512) for i in range(2)]
    vr = [uv(10240 + i * 2048, 1024) for i in range(2)]
    sbl = [uv(14336 + i * 1024, 512).rearrange("p (m e) -> p m e", m=4) for i in range(2)]
    qT4 = [[uv(16384 + (i * 4 + v) * 1024, 512).rearrange("p (m t) -> p m t", m=4) for v in range(4)] for i in range(2)]
    R_ROT = 24576
    qraw = uv(28672, 512, F32)
    qrot = uv(30720, 512, F32).rearrange("p (a d) -> p a d", a=8)
    qvar = [uv(32768 + v * 1024, 512).rearrange("p (m hh d) -> p m hh d", m=4, hh=2) for v in range(3)]
    kdf2 = [uv(35840 + i * 1024, 512).rearrange("p (m hh d) -> p m hh d", m=4, hh=2) for i in range(2)]
    smt = uv(37888, 1024).rearrange("p (h i) -> p h i", h=8)
    osq = uv(39936, 1024, F32).rearrange("p (h e) -> p h e", h=8)
    ot = osq
    mixr = uv(44032, 1024)
    yv = uv(0, 4 * D, F32).rearrange("p (t n) -> p t n", t=4)

    sfsw = [0]

    def stage_a_gen(st_):
        xi_ = xload(st_ * 512)
        for hc in range(6):
            xi_next = xload(st_ * 512 + (hc + 1) * 128) if hc + 1 < 6 else None
            rms_tile(xi_, hc)
            xi_ = xi_next
            yield

    for st in range(NST):
        if st == 0:
            for _ in stage_a_gen(0):
                pass
        hk_main = ["hT%d" % i for i in range(1, 5)]
        hk_all = ["hT%d" % i for i in range(6)]
        for half in range(2):
            Wb, wk = wnext()
            for c4 in range(4):
                ct = half * 4 + c4
                g, gk = gemm_feat(Wb, wk, c4 * 128, 128, 512, hk_main)
                S.op("act", lambda e, g=g, ct=ct: e.activation(out=qaT[:, ct, :], in_=g[:], func=AF.Copy, scale=0.125),
                     [gk], ["qaT"])
        Wb, wk = wnext()
        for ct in range(2):
            for (t0, nt) in ((0, 512), (512, 256)):
                g, gk = gemm_feat(Wb, wk, ct * 128, t0, nt, hk_all)
                S.op("act", lambda e, g=g, ct=ct, t0=t0, nt=nt: e.activation(
                    out=kaT[:, ct, t0:t0 + nt], in_=g[:, 0:nt], func=AF.Copy), [gk], ["kaT"])
        S.op("pool", lambda e: e.memset(va[:, :, :, 64:65], 1.0), [], ["va"])
        for hc in range(6):
            g, gk = gemm_tok(Wb, wk, 256, 256, hc)
            S.op("act", lambda e, g=g, hc=hc: e.activation(
                out=va[:, hc, :, 0:64], in_=g[:, 0:256].rearrange("p (k d) -> p k d", k=4), func=AF.Copy),
                [gk], ["va"])
        for half in range(2):
            Wb, wk = wnext()
            for t in range(4):
                g, gk = gemm_tok(Wb, wk, 0, 512, t + 1)
                S.op("act", lambda e, g=g, t=t, half=half: e.activation(
                    out=ga[:, t, half * 512:(half + 1) * 512], in_=g[:], func=AF.Silu), [gk], ["ga%d" % t])
        def att_scores(t, kvh):
            first = (st == 0 and t == 0)
            last = (st == NST - 1 and t == 3)
            kp, b0 = kvh // 2, (kvh % 2) * 64
            pts = []
            for kb in range(3):
                ai = (kvh * 3 + kb) % 2
                kt = t + kb
                S.op("pe", lambda e, ai=ai, kp=kp, b0=b0, kt=kt, t=t: e.matmul(
                    out=A[ai][:].rearrange("p (g i) -> p g i", g=4),
                    lhsT=kaT[b0:b0 + 64, kp, kt * 128:(kt + 1) * 128],
                    rhs=qaT[b0:b0 + 64, kp * 4:kp * 4 + 4, t * 128:(t + 1) * 128], start=True, stop=True),
                    ["kaT", "qaT"], ["A%d" % ai])
                bias_ap = biasT[:, kb, kvh * 4:kvh * 4 + 4, :]
                src = A[ai][:].rearrange("p (g i) -> p g i", g=4)
                if (kb == 0 and first) or (kb == 2 and last):
                    ec = edge[:, 0:1] if kb == 0 else edge[:, 1:2]
                    S.op("dve", lambda e, ai=ai, src=src, ec=ec, bias_ap=bias_ap: e.scalar_tensor_tensor(
                        out=tsc[ai], in0=src, scalar=ec, in1=bias_ap, op0=ALU.add, op1=ALU.add),
                        ["A%d" % ai, "biasT", "edge"], ["tsc%d" % ai])
                else:
                    S.op("dve", lambda e, ai=ai, src=src, bias_ap=bias_ap: e.tensor_tensor(
                        out=tsc[ai], in0=src, in1=bias_ap, op=ALU.add), ["A%d" % ai, "biasT"], ["tsc%d" % ai])
                pi_ = (kvh % 2) * 3 + kb
                S.op("act", lambda e, ai=ai, pi_=pi_: e.activation(out=PT[pi_], in_=tsc[ai], func=AF.Exp),
                     ["tsc%d" % ai], ["PT%d" % pi_])
                pts.append(pi_)
            return pts

        def att_pv(t, kvh, pts):
            po = A[2 + kvh % 2]
            pok = "A%d" % (2 + kvh % 2)
            pov = po[:, 0:264].rearrange("p (g d) -> p g d", g=4)
            for g_ in range(4):
                for kb in range(3):
                    kt = t + kb
                    S.op("pe", lambda e, g_=g_, kb=kb, kt=kt, pov=pov, kvh=kvh, pts=pts: e.matmul(
                        out=pov[:, g_, 0:65], lhsT=PT[pts[kb]][:, g_, :], rhs=va[:, kt, kvh, 0:65],
                        start=(kb == 0), stop=(kb == 2)),
                        ["PT%d" % pts[kb], "va"], [pok], inc=(g_ == 3 and kb == 2))
            den, denk = col8()
            S.op("dve", lambda e, pov=pov, den=den, kvh=kvh: e.tensor_tensor(
                out=den[:, 0:4], in0=pov[:, :, 64], in1=esink[:, kvh * 4:kvh * 4 + 4], op=ALU.add),
                [pok, "esink"], denk[0:4])
            S.op("dve", lambda e, den=den: e.reciprocal(out=den[:, 4:8], in_=den[:, 0:4]), denk[0:4], denk[4:8])
            S.op("dve", lambda e, pov=pov, den=den, kvh=kvh: e.tensor_tensor(
                out=atto[:, kvh * 256:(kvh + 1) * 256].rearrange("p (g d) -> p g d", g=4), in0=pov[:, :, 0:64],
                in1=den[:, 4:8].unsqueeze(2).to_broadcast([128, 4, 64]), op=ALU.mult),
                [pok] + denk[4:8], ["atto"])

        def att_tail(t):
            ssq, ssk = col()
            S.op("act", lambda e, ssq=ssq: e.activation(out=junk[:, 0:1024], in_=atto, func=AF.Square, accum_out=ssq),
                 ["atto"], [ssk])
            rs, rsk = rstd_from(ssq, ssk, 1024)
            S.op("dve", lambda e, rs=rs, t=t: e.scalar_tensor_tensor(
                out=mixa, in0=atto, scalar=rs, in1=ga[:, t, :], op0=ALU.mult, op1=ALU.mult),
                ["atto", rsk, "ga%d" % t], ["mixa"])
            for k in range(8):
                S.op("pe", lambda e, k=k: e.transpose(out=T[0][:, k, :], in_=mixa[:, k * 128:(k + 1) * 128],
                                                      identity=ident[:]), ["mixa", "ident"], ["T0"], inc=(k == 7))
            S.op("dve", lambda e, t=t: e.tensor_tensor(
                out=mixT[:, 0:8, t * 128:(t + 1) * 128], in0=T[0][:],
                in1=aonw[:].unsqueeze(2).to_broadcast([128, 8, 128]), op=ALU.mult), ["T0", "aonw"], ["mixTa%d" % t])

        pend = att_scores(0, 0)
        for t in range(4):
            for kvh in range(4):
                cur = pend
                if kvh < 3:
                    pend = att_scores(t, kvh + 1)
                elif t < 3:
                    pend = att_scores(t + 1, 0)
                att_pv(t, kvh, cur)
            att_tail(t)
        S.barrier()
        if STOP == "attn":
            return finish()

        ROT_OFF = R_ROT
        for half in range(2):
            Wg, wgk = wnext()
            for t in range(4):
                g, gk = gemm_tok(Wg, wgk, 0, 512, t + 1)
                S.op("act", lambda e, g=g, t=t, half=half: e.activation(
                    out=gr[:, t, half * 512:(half + 1) * 512], in_=g[:], func=AF.Silu), [gk], ["gr%d" % t])
        Wq, wqk = wnext()

        def ret_front(t):
            c = st * 4 + t
            b = c % 2
            S.dma("sp", lambda e, b=b, c=c: e.dma_start(out=ktok[b], in_=kscr[c]), ["kscr%d" % c], ["ktok%d" % b], "ldk%d" % b)
            S.dma("sp", lambda e, b=b, c=c: e.dma_start(out=vr[b], in_=vscr[c]), ["vscr%d" % c], ["vr%d" % b], "ldv%d" % b)
            S.dma("sp", lambda e, b=b, c=c: e.dma_start(out=sbl[b].rearrange("p m e -> p (m e)"), in_=sbscr[c]),
                  ["sbscr%d" % c], ["sbl%d" % b], "lds%d" % b)
            g, gk = gemm_tok(Wq, wqk, 0, 512, t + 1)
            S.op("act", lambda e, g=g: e.activation(out=qraw, in_=g[:], func=AF.Copy), [gk], ["qraw"])
            rotary(qraw, "qraw", qrot, "qrot", c)
            qrot4 = qrot.rearrange("p (m hh) d -> p m hh d", hh=2)
            S.op(EW2, lambda e: e.tensor_copy(out=qvar[0], in_=qrot4), ["qrota", "qrotb"], ["qvar0"])
            S.op(EW2, lambda e: e.tensor_tensor(out=qvar[1], in0=qrot4, in1=pi_bcast(qdecf[:]), op=ALU.mult),
                 ["qrota", "qrotb", "dt0"], ["qvar1"])
            S.op(EW2, lambda e: e.tensor_tensor(out=qvar[2], in0=qrot4, in1=pi_bcast(qdecb[:]), op=ALU.mult),
                 ["qrota", "qrotb", "dt1"], ["qvar2"])
            ktv = ktok[b].rearrange("p (m hh d) -> p m hh d", m=4, hh=2)
            S.op(EW2, lambda e, b=b, ktv=ktv: e.tensor_tensor(out=kdf2[b], in0=ktv, in1=pi_bcast(kdecf[:]), op=ALU.mult),
                 ["ktok%d" % b, "dt2"], ["kdf2%d" % b])
            srcs = [(qvar[0], "qvar0"), (qvar[1], "qvar1"), (qvar[2], "qvar2"), (ktv, "ktok%d" % b)]
            for v, (sv_, svk) in enumerate(srcs):
                ti = v % 2
                for m in range(4):
                    S.op("pe", lambda e, sv_=sv_, m=m, ti=ti, v=v: e.transpose(
                        out=T[ti][:, (v // 2) * 4 + m, :], in_=sv_[:, m].rearrange("p hh d -> p (hh d)"),
                        identity=ident[:]), [svk, "ident"], ["T%d" % ti], inc=(m == 3 and v >= 2))
            for ti in range(2):
                for w in range(2):
                    v = w * 2 + ti
                    eng = "dve"
                    if eng == "act":
                        S.op("act", lambda e, ti=ti, w=w, v=v, b=b: e.activation(
                            out=qT4[b][v], in_=T[ti][:, w * 4:(w + 1) * 4, :], func=AF.Copy), ["T%d" % ti], ["qT4%d%d" % (b, v)])
                    else:
                        S.op("dve", lambda e, ti=ti, w=w, v=v, b=b: e.tensor_copy(
                            out=qT4[b][v], in_=T[ti][:, w * 4:(w + 1) * 4, :]), ["T%d" % ti], ["qT4%d%d" % (b, v)])
            for m in range(4):
                S.op("dve", lambda e, m=m, b=b, c=c: e.scalar_tensor_tensor(
                    out=sbc[b][:, m, :], in0=Sin32[:, 1, m, :], scalar=pwb[:, c, m:m + 1], in1=sbl[b][:, m, :],
                    op0=ALU.mult, op1=ALU.add), ["Sin32", "pwb", "sbl%d" % b], ["sbc%d" % b])

        def ret_back(t):
            c = st * 4 + t
            b = c % 2
            qpT, qfT, qbT, kT = qT4[b]
            for h in range(8):
                hh, m = h // 4, h % 4
                S.op("pe", lambda e, h=h, hh=hh, m=m, kT=kT, qpT=qpT: e.matmul(
                    out=A[h // 4][:, (h % 4) * 128:(h % 4 + 1) * 128], lhsT=kT[hh * 64:(hh + 1) * 64, m, :],
                    rhs=qpT[hh * 64:(hh + 1) * 64, m, :], start=True, stop=True),
                    ["qT4%d3" % b, "qT4%d0" % b], ["A%d" % (h // 4)], inc=(h % 4 == 3))
            for a_ in range(2):
                S.op("dve", lambda e, a_=a_: e.tensor_tensor(
                    out=smt[:, a_ * 4:(a_ + 1) * 4, :], in0=A[a_][:].rearrange("p (h i) -> p h i", h=4),
                    in1=dmaskT[:, a_ * 4:(a_ + 1) * 4, :], op=ALU.mult), ["A%d" % a_, "dmaskT"], ["smt"])
            sfi = sfsw[0] % 2
            for h in range(8):
                hh, m = h // 4, h % 4
                pr = slice(hh * 64, (hh + 1) * 64)
                oap = A[2 + h // 4][:, (h % 4) * 128:(h % 4 + 1) * 128]
                ok_ = "A%d" % (2 + h // 4)
                S.op("pe", lambda e, h=h, oap=oap, b=b: e.matmul(out=oap, lhsT=smt[:, h, :], rhs=vr[b][:, h * 128:(h + 1) * 128],
                                                                 start=True, stop=False), ["smt", "vr%d" % b], [ok_], inc=False)
                S.op("pe", lambda e, m=m, pr=pr, oap=oap, qfT=qfT, sfi=sfi: e.matmul(
                    out=oap, lhsT=qfT[pr, m, :], rhs=Sfb[sfi][pr, m, :], start=False, stop=False),
                    ["qT4%d1" % b, "Sfb%d" % sfi], [ok_], inc=False)
                S.op("pe", lambda e, m=m, pr=pr, oap=oap, qbT=qbT, b=b: e.matmul(
                    out=oap, lhsT=qbT[pr, m, :], rhs=sbc[b][pr, m, :], start=False, stop=True),
                    ["qT4%d2" % b, "sbc%d" % b], [ok_], inc=(h % 4 == 3))
            for hh in range(2):
                for m in range(4):
                    h = 4 * hh + m
                    S.op("pe", lambda e, b=b, hh=hh, m=m, h=h: e.matmul(
                        out=G[hh][:, m * 128:(m + 1) * 128], lhsT=kdf2[b][:, m].rearrange("p hh d -> p (hh d)"),
                        rhs=vr[b][:, h * 128:(h + 1) * 128], start=True, stop=True),
                        ["kdf2%d" % b, "vr%d" % b], ["G%d" % hh], inc=(m == 3))
            gcnt[0] = 0
            S.op("dve", lambda e: e.tensor_tensor(
                out=Sf32[:], in0=Sf32[:], in1=DC[:, 0:4].unsqueeze(2).to_broadcast([128, 4, 128]), op=ALU.mult),
                ["Sf32", "DC"], ["Sf32"])
            for hh in range(2):
                pr = slice(hh * 64, (hh + 1) * 64)
                S.op("dve", lambda e, hh=hh, pr=pr: e.tensor_tensor(
                    out=Sf32[pr], in0=Sf32[pr], in1=G[hh][pr, :].rearrange("p (m e) -> p m e", m=4), op=ALU.add),
                    ["Sf32", "G%d" % hh], ["Sf32"])
            sfsw[0] += 1
            sfn = sfsw[0] % 2
            S.op("act", lambda e, sfn=sfn: e.activation(out=Sfb[sfn][:], in_=Sf32[:], func=AF.Copy), ["Sf32"], ["Sfb%d" % sfn])
            s1, s1k = col8()
            s2, s2k = col8()
            for a_ in range(2):
                ov = A[2 + a_][:].rearrange("p (h e) -> p h e", h=4)
                S.op("act", lambda e, a_=a_, ov=ov: e.activation(out=osq[:, a_ * 4:(a_ + 1) * 4, :], in_=ov, func=AF.Square),
                     ["A%d" % (2 + a_)], ["ot%d" % a_])
                S.op("dve", lambda e, a_=a_, ov=ov, s1=s1: e.tensor_reduce(out=s1[:, a_ * 4:(a_ + 1) * 4], in_=ov, axis=AX.X, op=ALU.add),
                     ["A%d" % (2 + a_)], s1k[a_ * 4:(a_ + 1) * 4])
                S.op("dve", lambda e, a_=a_, s2=s2: e.tensor_reduce(out=s2[:, a_ * 4:(a_ + 1) * 4], in_=osq[:, a_ * 4:(a_ + 1) * 4, :],
                                                                     axis=AX.X, op=ALU.add), ["ot%d" % a_], s2k[a_ * 4:(a_ + 1) * 4])
            mean, meank = col8()
            var, vark = col8()
            S.op("dve", lambda e, mean=mean, s1=s1: e.tensor_single_scalar(out=mean, in_=s1, scalar=1.0 / 128, op=ALU.mult),
                 s1k, meank)
            S.op("dve", lambda e, var=var, mean=mean: e.tensor_tensor(out=var, in0=mean, in1=mean, op=ALU.mult), meank, vark)
            S.op("dve", lambda e, var=var, s2=s2: e.scalar_tensor_tensor(out=var, in0=s2, scalar=1.0 / 128, in1=var,
                                                                         op0=ALU.mult, op1=ALU.subtract), s2k + vark, vark)
            S.op("act", lambda e, var=var: e.activation(out=var, in_=var, func=AF.Sqrt, bias=EPS), vark, vark)
            S.op("dve", lambda e, var=var: e.reciprocal(out=var, in_=var), vark, vark)
            for a_ in range(2):
                ov = A[2 + a_][:].rearrange("p (h e) -> p h e", h=4)
                S.op("dve", lambda e, a_=a_, ov=ov, mean=mean: e.tensor_tensor(
                    out=ot[:, a_ * 4:(a_ + 1) * 4, :], in0=ov,
                    in1=mean[:, a_ * 4:(a_ + 1) * 4].unsqueeze(2).to_broadcast([128, 4, 128]), op=ALU.subtract),
                    ["A%d" % (2 + a_)] + meank, ["ot%d" % a_])
            S.op("dve", lambda e, var=var: e.tensor_tensor(out=ot, in0=ot, in1=var.unsqueeze(2).to_broadcast([128, 8, 128]),
                                                           op=ALU.mult), ["ot0", "ot1"] + vark, ["ot0", "ot1"])
            S.op(EW2, lambda e, t=t: e.tensor_tensor(out=mixr.rearrange("p (h e) -> p h e", h=8), in0=ot,
                                                        in1=gr[:, t, :].rearrange("p (h e) -> p h e", h=8), op=ALU.mult),
                 ["ot0", "ot1", "gr%d" % t], ["mixr"])
            for k in range(8):
                S.op("pe", lambda e, k=k: e.transpose(out=T[0][:, k, :], in_=mixr[:, k * 128:(k + 1) * 128],
                                                      identity=ident[:]), ["mixr", "ident"], ["T0"], inc=(k == 7))
            S.op("dve", lambda e, t=t: e.tensor_tensor(
                out=mixT[:, 8:16, t * 128:(t + 1) * 128], in0=T[0][:],
                in1=gnw[:].unsqueeze(2).to_broadcast([128, 8, 128]), op=ALU.mult), ["T0", "gnw"], ["mixTr%d" % t])

        ret_front(0)
        for t in range(4):
            if t < 3:
                ret_front(t + 1)
            ret_back(t)
        S.barrier()
        if STOP == "ret":
            return finish()

        for t in range(4):
            xres = x_d[128 + (st * 4 + t) * 128:128 + (st * 4 + t + 1) * 128, :]
            S.dma("sp", lambda e, t=t, xres=xres: e.dma_start(out=yv[:, t, :], in_=xres),
                  [], ["y%d" % t], "ldy%d" % t)
        nxt = stage_a_gen(st + 1) if st + 1 < NST else None
        for ob in range(4):
            if nxt is not None and ob > 0:
                for _ in range((2, 2, 2)[ob - 1]):
                    next(nxt, None)
            Wb, wk = wnext()
            for t in range(4):
                gi = gcnt[0] % 2
                gcnt[0] += 1
                for kc in range(KC):
                    S.op("pe", lambda e, kc=kc, gi=gi, t=t, Wb=Wb: e.matmul(
                        out=G[gi][:], lhsT=mixT[:, kc, t * 128:(t + 1) * 128], rhs=Wb[:, kc, :],
                        start=(kc == 0), stop=(kc == KC - 1)),
                        [wk, "mixTa%d" % t, "mixTr%d" % t], ["G%d" % gi], inc=(kc == KC - 1))
                S.op("dve", lambda e, gi=gi, t=t, ob=ob: e.tensor_tensor(
                    out=yv[:, t, ob * 512:(ob + 1) * 512], in0=G[gi][:], in1=yv[:, t, ob * 512:(ob + 1) * 512], op=ALU.add),
                    ["G%d" % gi, "y%d" % t], ["y%d" % t])
        for t in range(4):
            ssq, ssk = col()
            S.op("act", lambda e, t=t, ssq=ssq: e.activation(out=junk[:], in_=yv[:, t, :], func=AF.Square, accum_out=ssq),
                 ["y%d" % t], [ssk])
            rs, rsk = rstd_from(ssq, ssk, D)
            S.op("dve", lambda e, t=t, rs=rs: e.scalar_tensor_tensor(
                out=yv[:, t, :], in0=yv[:, t, :], scalar=rs, in1=fnw[:], op0=ALU.mult, op1=ALU.mult),
                ["y%d" % t, rsk, "fnw"], ["y%d" % t])
            ydst = y_d[(st * 4 + t) * 128:(st * 4 + t + 1) * 128, :]
            S.dma("pool", lambda e, t=t, ydst=ydst: e.dma_start(out=ydst, in_=yv[:, t, :]),
                  ["y%d" % t], ["yout"], "sty%d" % t)
        S.barrier()
        if STOP == "k%d" % (st + 1):
            return finish()

    return finish()


def _t5_bucket(rel):
    nb, me = 16, 8
    ret = np.where(rel > 0, nb, 0)
    n = np.abs(rel)
    nf = np.maximum(n, 1).astype(np.float32)
    val = (np.log(nf / np.float32(me)) / np.float32(math.log(128 / me)) * np.float32(nb - me)).astype(np.float32)
    large = np.minimum(me + val.astype(np.int32), nb - 1)
    return ret + np.where(n < me, n, large)


_CACHE = {}


def kernel(x, norm_w, w_in, attn_sink, rel_bias, attn_out_norm_w, ret_decay_fwd, ret_decay_bwd,
           ret_gn_w, w_out, final_norm_w):
    x = np.asarray(x, np.float32)
    B, SEQ, _ = x.shape
    assert B == 2
    TC = SEQ // 4
    NT = TC // 128
    w_in = np.asarray(w_in, np.float32)[0]
    w_out = np.ascontiguousarray(np.asarray(w_out, np.float32)[0])
    qperm = []
    for kp in range(2):
        for g in range(4):
            for hs in range(2):
                head = 4 * (2 * kp + hs) + g
                qperm += list(range(head * 64, head * 64 + 64))
    wA = np.ascontiguousarray(np.concatenate(
        [w_in[:, qperm], w_in[:, 1024:1280], w_in[:, 1280:1536], w_in[:, 1536:2560], w_in[:, 4608:5632],
         w_in[:, 2560:3072]], axis=1))
    wKV = np.ascontiguousarray(w_in[:, 3072:4608])
    j = np.arange(128)[:, None, None]
    kb = np.arange(3)[None, :, None]
    i = np.arange(128)[None, None, :]
    rel = (kb - 1) * 128 + j - i
    bucket = _t5_bucket(rel.astype(np.int32))
    rb = np.asarray(rel_bias, np.float32)
    biasg = np.ascontiguousarray(np.transpose(rb[bucket], (0, 1, 3, 2)))
    maskc = np.where(np.abs(rel) > 128, NEG, 0.0).astype(np.float32)
    inv = (np.float32(10000.0) ** (-(np.arange(0, 64, 2, dtype=np.float32)) / np.float32(64))).astype(np.float32)
    tr = lambda v, n: np.ascontiguousarray(np.asarray(v, np.float32).reshape(n, 128).T)
    shared = {
        "wA": wA, "wKV": wKV, "wO": w_out, "biasg": biasg, "maskc": maskc,
        "normw_t": tr(norm_w, 16), "aonw_t": tr(attn_out_norm_w, 8), "gnw_t": tr(ret_gn_w, 8),
        "fnw": np.asarray(final_norm_w, np.float32).reshape(-1),
        "sink": np.asarray(attn_sink, np.float32).reshape(-1),
        "dec": np.concatenate([np.asarray(ret_decay_fwd, np.float32).reshape(-1),
                               np.asarray(ret_decay_bwd, np.float32).reshape(-1)]),
    }
    in_maps = []
    for c in range(8):
        b, r = c // 4, c % 4
        xc = np.zeros((TC + 256, D), np.float32)
        lo, hi = r * TC - 128, (r + 1) * TC + 128
        slo, shi = max(lo, 0), min(hi, SEQ)
        xc[slo - lo:shi - lo] = x[b, slo:shi]
        pos = (r * TC + np.arange(TC)).astype(np.float32)
        ang = (pos[:, None] * inv[None, :]).astype(np.float32).astype(np.float64)
        cs = np.stack([np.cos(ang), np.sin(ang)], 0).astype(np.float32)
        rot = np.ascontiguousarray(cs.reshape(2, NT, 128, 32).transpose(2, 0, 1, 3))
        edge = np.zeros((128, 2), np.float32)
        edge[:, 0] = NEG if r == 0 else 0.0
        edge[:, 1] = NEG if r == 3 else 0.0
        cc = np.zeros(32, np.float32)
        for s in range(8):
            if s // 4 != b:
                continue
            rs = s % 4
            if rs < r:
                cc[s] = r - 1 - rs
                cc[16 + s] = 1.0
            if rs > r:
                cc[8 + s] = rs - r - 1
                cc[24 + s] = 1.0
        m = dict(shared)
        m.update({"x": xc, "rot": rot, "edge": edge, "cc": cc})
        in_maps.append(m)
    if TC not in _CACHE:
        _CACHE[TC] = build(TC)
    res = run_bass_kernel_spmd(_CACHE[TC], in_maps, core_ids=list(range(8)))
    out = np.empty((B, SEQ, D), np.float32)
    for c in range(8):
        b, r = c // 4, c % 4
        out[b, r * TC:(r + 1) * TC] = res.results[c]["y"]
    return out
```
